# Optimizing a Trainium2 kernel written in Bass

```python
import math
import jax, jax.numpy as jnp
from jax import lax
import numpy as np

D_MODEL = 1024
BATCH = 8
SEQ = 4096
DEPTH = 2

HEAD_DIM = 64
ROT_DIM = HEAD_DIM // 4
ROPE_THETA = 500000.0
RMS_EPS = 1e-6
CONV_WIDTH = D_MODEL // 2
CONV_K = 3
DIFF_HEADS = (D_MODEL // 2) // (2 * HEAD_DIM)
DIFF_VDIM = 2 * HEAD_DIM
DIFF_WIDTH = DIFF_HEADS * 2 * HEAD_DIM
DIFF_EPS = 1e-5
QBLK = 128
EVEN_IN = 3 * CONV_WIDTH + 3 * DIFF_WIDTH
SWA_HEADS = D_MODEL // HEAD_DIM
SWA_GROUP = 8
SWA_KV_HEADS = SWA_HEADS // SWA_GROUP
WINDOW = 128
ODD_IN = (SWA_HEADS + 2 * SWA_KV_HEADS) * HEAD_DIM
D_FF = 4 * D_MODEL
N_EVEN = (DEPTH + 1) // 2
N_ODD = DEPTH // 2

kernel_name = "hybrid_conv_diffattn_swa_sink_trunk"


def rmsnorm(x, w, eps=RMS_EPS):
    xf = x.astype(jnp.float32)
    xf = xf * lax.rsqrt(jnp.mean(xf * xf, axis=-1, keepdims=True) + eps)
    return (xf * w.astype(jnp.float32)).astype(x.dtype)


def rope_tables(positions):
    inv_freq = ROPE_THETA ** (-jnp.arange(0, ROT_DIM, 2, dtype=jnp.float32) / ROT_DIM)
    ang = positions.astype(jnp.float32)[..., None] * inv_freq
    return jnp.cos(ang), jnp.sin(ang)


def apply_partial_rope(x, cos, sin):
    half = ROT_DIM // 2
    extra = x.ndim - 3
    c = cos.reshape(cos.shape[:2] + (1,) * extra + (half,))
    s = sin.reshape(sin.shape[:2] + (1,) * extra + (half,))
    xr = x[..., :ROT_DIM].astype(jnp.float32)
    x1, x2 = xr[..., :half], xr[..., half:]
    rot = jnp.concatenate([x1 * c - x2 * s, x2 * c + x1 * s], axis=-1)
    return jnp.concatenate([rot.astype(x.dtype), x[..., ROT_DIM:]], axis=-1)


def causal_short_conv(u, w):
    S = u.shape[1]
    up = jnp.pad(u, ((0, 0), (CONV_K - 1, 0), (0, 0)))
    return sum(w[i] * up[:, i:i + S] for i in range(CONV_K))


def diff_attention(q, k, v, lam, subln_w, lam_init):
    B, S, H, _, d = q.shape
    nb = S // QBLK
    scale = d ** -0.5
    qb = jnp.moveaxis(q.reshape(B, nb, QBLK, H, 2, d), 1, 0)
    kpos = jnp.arange(S)

    def one_block(args):
        blk, qblk = args
        s = jnp.einsum('bqhcd,bkhcd->bhcqk', qblk, k,
                       preferred_element_type=jnp.float32) * scale
        qpos = blk * QBLK + jnp.arange(QBLK)
        mask = kpos[None, :] <= qpos[:, None]
        p = jax.nn.softmax(jnp.where(mask, s, -jnp.inf), axis=-1)
        a = p[:, :, 0] - lam[None, None, None, None] * p[:, :, 1]
        return jnp.einsum('bhqk,bkhe->bqhe', a.astype(v.dtype), v)

    o = lax.map(one_block, (jnp.arange(nb), qb))
    o = jnp.moveaxis(o, 0, 1).reshape(B, S, H, 2 * d)
    o = rmsnorm(o, subln_w, DIFF_EPS) * (1.0 - lam_init)
    return o.reshape(B, S, H * 2 * d)


def even_mixer(h, cos, sin, w_in, conv_w, lq1, lk1, lq2, lk2, subln_w, w_out, lam_init):
    B, S, _ = h.shape
    proj = h @ w_in
    c0 = CONV_WIDTH
    gb, gc, xc, q, k, v = jnp.split(
        proj, [c0, 2 * c0, 3 * c0, 3 * c0 + DIFF_WIDTH, 3 * c0 + 2 * DIFF_WIDTH], axis=-1)
    conv_out = gb * causal_short_conv(gc * xc, conv_w)
    q = apply_partial_rope(q.reshape(B, S, DIFF_HEADS, 2, HEAD_DIM), cos, sin)
    k = apply_partial_rope(k.reshape(B, S, DIFF_HEADS, 2, HEAD_DIM), cos, sin)
    v = v.reshape(B, S, DIFF_HEADS, DIFF_VDIM)
    f32 = jnp.float32
    lam = (jnp.exp(jnp.sum(lq1.astype(f32) * lk1.astype(f32)))
           - jnp.exp(jnp.sum(lq2.astype(f32) * lk2.astype(f32))) + lam_init)
    diff_out = diff_attention(q, k, v, lam, subln_w, lam_init)
    return jnp.concatenate([conv_out, diff_out], axis=-1) @ w_out


def band(t, nb):
    B, S, KV, d = t.shape
    tp = jnp.pad(t, ((0, 0), (WINDOW, 0), (0, 0), (0, 0)))
    prev = tp[:, :S].reshape(B, nb, WINDOW, KV, d)
    cur = t.reshape(B, nb, WINDOW, KV, d)
    return jnp.concatenate([prev, cur], axis=2)


def sliding_window_attention(q, k, v, sinks):
    B, S, H, d = q.shape
    nb = S // WINDOW
    qb = q.reshape(B, nb, WINDOW, SWA_KV_HEADS, SWA_GROUP, d)
    kb, vb = band(k, nb), band(v, nb)
    s = jnp.einsum('bnqkgd,bnjkd->bnkgqj', qb, kb,
                   preferred_element_type=jnp.float32) * (d ** -0.5)
    i = jnp.arange(WINDOW)[:, None]
    j = jnp.arange(2 * WINDOW)[None, :]
    dist = i + WINDOW - j
    in_band = (dist >= 0) & (dist < WINDOW)
    key_valid = (jnp.arange(nb)[:, None, None] * WINDOW - WINDOW + j[None]) >= 0
    mask = in_band[None] & key_valid
    s = jnp.where(mask[None, :, None, None], s, -jnp.inf)
    sink = sinks.astype(jnp.float32).reshape(SWA_KV_HEADS, SWA_GROUP)[None, None, :, :, None, None]
    sink = jnp.broadcast_to(sink, s.shape[:-1] + (1,))
    p = jax.nn.softmax(jnp.concatenate([s, sink], axis=-1), axis=-1)[..., :-1]
    o = jnp.einsum('bnkgqj,bnjkd->bnqkgd', p.astype(v.dtype), vb)
    return o.reshape(B, S, H * d)


def odd_mixer(h, cos, sin, w_qkv, b_qkv, sinks, w_o, b_o):
    B, S, _ = h.shape
    proj = h @ w_qkv + b_qkv
    nq = SWA_HEADS * HEAD_DIM
    nk = SWA_KV_HEADS * HEAD_DIM
    q, k, v = jnp.split(proj, [nq, nq + nk], axis=-1)
    q = apply_partial_rope(q.reshape(B, S, SWA_HEADS, HEAD_DIM), cos, sin)
    k = apply_partial_rope(k.reshape(B, S, SWA_KV_HEADS, HEAD_DIM), cos, sin)
    v = v.reshape(B, S, SWA_KV_HEADS, HEAD_DIM)
    return sliding_window_attention(q, k, v, sinks) @ w_o + b_o


def squared_relu_mlp(h, w1, w2):
    return jnp.square(jax.nn.relu(h @ w1)) @ w2


def setup_inputs(seed: int = 0) -> dict:
    key = jax.random.key(seed)
    ks = jax.random.split(key, 24)
    f32 = jnp.float32

    def nrm(k, shape, scale):
        return jax.random.normal(k, shape, f32) * scale

    def gain(k, shape):
        return 1.0 + 0.05 * jax.random.normal(k, shape, f32)

    offset = jax.random.randint(ks[1], (BATCH, 1), 0, SEQ, dtype=jnp.int32)
    positions = offset + jnp.arange(SEQ, dtype=jnp.int32)[None, :]
    return {
        "x": nrm(ks[0], (BATCH, SEQ, D_MODEL), 1.0),
        "positions": positions,
        "norm_pre_mix": gain(ks[2], (DEPTH, D_MODEL)),
        "norm_post_mix": gain(ks[3], (DEPTH, D_MODEL)),
        "norm_pre_mlp": gain(ks[4], (DEPTH, D_MODEL)),
        "norm_post_mlp": gain(ks[5], (DEPTH, D_MODEL)),
        "even_w_in": nrm(ks[6], (N_EVEN, D_MODEL, EVEN_IN), D_MODEL ** -0.5),
        "even_conv_w": nrm(ks[7], (N_EVEN, CONV_K, CONV_WIDTH), CONV_K ** -0.5),
        "even_lambda_q1": nrm(ks[8], (N_EVEN, HEAD_DIM), 0.1),
        "even_lambda_k1": nrm(ks[9], (N_EVEN, HEAD_DIM), 0.1),
        "even_lambda_q2": nrm(ks[10], (N_EVEN, HEAD_DIM), 0.1),
        "even_lambda_k2": nrm(ks[11], (N_EVEN, HEAD_DIM), 0.1),
        "even_subln_w": gain(ks[12], (N_EVEN, DIFF_VDIM)),
        "even_w_out": nrm(ks[13], (N_EVEN, D_MODEL, D_MODEL), D_MODEL ** -0.5),
        "odd_w_qkv": nrm(ks[14], (N_ODD, D_MODEL, ODD_IN), D_MODEL ** -0.5),
        "odd_b_qkv": nrm(ks[15], (N_ODD, ODD_IN), 0.02),
        "odd_sinks": nrm(ks[16], (N_ODD, SWA_HEADS), 0.5),
        "odd_w_o": nrm(ks[17], (N_ODD, D_MODEL, D_MODEL), D_MODEL ** -0.5),
        "odd_b_o": nrm(ks[18], (N_ODD, D_MODEL), 0.02),
        "mlp_w1": nrm(ks[19], (DEPTH, D_MODEL, D_FF), D_MODEL ** -0.5),
        "mlp_w2": nrm(ks[20], (DEPTH, D_FF, D_MODEL), D_FF ** -0.5),
    }


def reference(x, positions, norm_pre_mix, norm_post_mix, norm_pre_mlp, norm_post_mlp,
              even_w_in, even_conv_w, even_lambda_q1, even_lambda_k1, even_lambda_q2,
              even_lambda_k2, even_subln_w, even_w_out, odd_w_qkv, odd_b_qkv, odd_sinks,
              odd_w_o, odd_b_o, mlp_w1, mlp_w2):
    cos, sin = rope_tables(positions)
    for l in range(DEPTH):
        h = rmsnorm(x, norm_pre_mix[l])
        if l % 2 == 0:
            e = l // 2
            lam_init = 0.8 - 0.6 * math.exp(-0.3 * l)
            h = even_mixer(h, cos, sin, even_w_in[e], even_conv_w[e], even_lambda_q1[e],
                           even_lambda_k1[e], even_lambda_q2[e], even_lambda_k2[e],
                           even_subln_w[e], even_w_out[e], lam_init)
        else:
            o = l // 2
            h = odd_mixer(h, cos, sin, odd_w_qkv[o], odd_b_qkv[o], odd_sinks[o],
                          odd_w_o[o], odd_b_o[o])
        x = x + rmsnorm(h, norm_post_mix[l])
        h = squared_relu_mlp(rmsnorm(x, norm_pre_mlp[l]), mlp_w1[l], mlp_w2[l])
        x = x + rmsnorm(h, norm_post_mlp[l])
    return x
```

```python
import numpy as np
import concourse.bass as bass
import concourse.mybir as mybir
from concourse.bass_utils import run_bass_kernel_spmd
from contextlib import ExitStack

F32 = mybir.dt.float32
BF16 = mybir.dt.bfloat16
I32 = mybir.dt.int32
ALU = mybir.AluOpType
AF = mybir.ActivationFunctionType

SEQ = 4096
D = 1024
T = 512
NCH = SEQ // T
TT = T // 128
DC = D // 128
NSLOT = 4
C1 = 6.28125
C2 = float(2 * np.pi - 6.28125)


class Reg:
    __slots__ = ("lw", "rd")

    def __init__(self):
        self.lw = None
        self.rd = []


class TV:
    __slots__ = ("ap", "regs")

    def __init__(self, ap, regs):
        self.ap = ap
        self.regs = tuple(regs)


class Op:
    __slots__ = ("eng", "fn", "stream", "deps", "mark", "milestone", "ninc")


def _ap(x):
    return x.ap if isinstance(x, TV) else x


def _rg(*xs):
    out = []
    for x in xs:
        if isinstance(x, TV):
            out.extend(x.regs)
    return out


class Sched:
    def __init__(self, nc, es):
        self.nc = nc
        self.es = es
        self.eng = {"pe": nc.tensor, "act": nc.scalar, "dve": nc.vector, "pool": nc.gpsimd, "sp": nc.sync}
        self.ops = {k: [] for k in self.eng}
        self.sems = {}
        self.nops = 0
        self._batch = {}

    def batch_begin(self, eng):
        self._batch[eng] = None

    def batch_end(self, eng):
        del self._batch[eng]

    def sem(self, key):
        if key not in self.sems:
            self.sems[key] = self.es.enter_context(self.nc.semaphore("s%d" % len(self.sems)))
        return self.sems[key]

    def op(self, eng, fn, reads=(), writes=(), dma=None, ninc=1):
        o = Op()
        o.eng = eng
        o.fn = fn
        o.stream = ("dma", dma) if dma is not None else eng
        o.mark = None
        o.milestone = dma is not None
        o.ninc = ninc
        self.nops += 1
        deps = set()
        for r in reads:
            if r.lw is not None:
                deps.add(r.lw)
        for w in writes:
            if w.lw is not None:
                deps.add(w.lw)
            deps.update(w.rd)
        deps.discard(o)
        o.deps = [d for d in deps if not (d.stream == "pe" and eng == "pe")]
        for r in reads:
            r.rd.append(o)
        for w in writes:
            w.lw = o
            w.rd = []
        if eng in self._batch:
            first = self._batch[eng]
            if first is None:
                self._batch[eng] = o
            else:
                first.deps.extend(o.deps)
                o.deps = []
        self.ops[eng].append(o)
        return o

    def emit(self, final_eng, final_ops):
        for lst in self.ops.values():
            for o in lst:
                for d in o.deps:
                    d.milestone = True
        cnt = {}
        for lst in self.ops.values():
            for o in lst:
                if o.milestone:
                    inc = 16 * o.ninc if isinstance(o.stream, tuple) else 1
                    cnt[o.stream] = cnt.get(o.stream, 0) + inc
                    o.mark = cnt[o.stream]
        for e, lst in self.ops.items():
            E = self.eng[e]
            known = {}
            for o in lst:
                need = {}
                for d in o.deps:
                    if d.mark > need.get(d.stream, 0):
                        need[d.stream] = d.mark
                for s, v in need.items():
                    if known.get(s, 0) >= v:
                        continue
                    known[s] = v
                    E.wait_ge(self.sem(s), v)
                ins = o.fn(E)
                if o.milestone:
                    if isinstance(o.stream, tuple):
                        if not isinstance(ins, (list, tuple)):
                            ins = [ins]
                        assert len(ins) == o.ninc
                        for i_ in ins:
                            i_.then_inc(self.sem(o.stream), 16)
                    else:
                        ins.then_inc(self.sem(o.stream), 1)
            if e == final_eng:
                need = {}
                for d in final_ops:
                    if d.mark > need.get(d.stream, 0):
                        need[d.stream] = d.mark
                for s, v in need.items():
                    E.wait_ge(self.sem(s), v)

    def act(self, out, in_, func, bias=None, scale=1.0, accum=None):
        kw = {}
        if bias is not None:
            kw["bias"] = _ap(bias)
        if accum is not None:
            kw["accum_out"] = _ap(accum)
        o_, i_, s_ = _ap(out), _ap(in_), _ap(scale)
        return self.op("act", lambda e: e.activation(out=o_, in_=i_, func=func, scale=s_, **kw),
                       _rg(in_, bias, scale), _rg(out, accum))

    def ts(self, eng, out, in0, s1, s2, op0, op1=None):
        o_, i_, a_, b_ = _ap(out), _ap(in0), _ap(s1), _ap(s2)
        if op1 is None:
            f = lambda e: e.tensor_scalar(out=o_, in0=i_, scalar1=a_, scalar2=None, op0=op0)
        else:
            f = lambda e: e.tensor_scalar(out=o_, in0=i_, scalar1=a_, scalar2=b_, op0=op0, op1=op1)
        return self.op(eng, f, _rg(in0, s1, s2), _rg(out))

    def tt(self, eng, out, in0, in1, op):
        o_, a_, b_ = _ap(out), _ap(in0), _ap(in1)
        return self.op(eng, lambda e: e.tensor_tensor(out=o_, in0=a_, in1=b_, op=op), _rg(in0, in1), _rg(out))

    def stt(self, eng, out, in0, scalar, in1, op0, op1):
        o_, a_, s_, b_ = _ap(out), _ap(in0), _ap(scalar), _ap(in1)
        return self.op(eng, lambda e: e.scalar_tensor_tensor(out=o_, in0=a_, scalar=s_, in1=b_, op0=op0, op1=op1),
                       _rg(in0, scalar, in1), _rg(out))

    def copy(self, eng, out, in_):
        o_, i_ = _ap(out), _ap(in_)
        if eng == "act":
            return self.op("act", lambda e: e.activation(out=o_, in_=i_, func=AF.Copy), _rg(in_), _rg(out))
        return self.op(eng, lambda e: e.tensor_copy(out=o_, in_=i_), _rg(in_), _rg(out))

    def recip(self, out, in_):
        o_, i_ = _ap(out), _ap(in_)
        return self.op("dve", lambda e: e.reciprocal(out=o_, in_=i_), _rg(in_), _rg(out))

    def memset(self, eng, out, val):
        o_ = _ap(out)
        return self.op(eng, lambda e: e.memset(o_, val), (), _rg(out))

    def mm(self, out, lhsT, rhs, start=True, stop=True):
        o_, l_, r_ = _ap(out), _ap(lhsT), _ap(rhs)
        return self.op("pe", lambda e: e.matmul(o_, lhsT=l_, rhs=r_, start=start, stop=stop), _rg(lhsT, rhs), _rg(out))

    def tr(self, out, in_, ident):
        o_, i_, d_ = _ap(out), _ap(in_), _ap(ident)
        return self.op("pe", lambda e: e.transpose(out=o_, in_=i_, identity=d_), _rg(in_, ident), _rg(out))

    def dma(self, q, outs_ins, reads, writes, key):
        pairs = [(_ap(a), _ap(b)) for a, b in outs_ins]
        return self.op(q, lambda e: [e.dma_start(out=a, in_=b) for a, b in pairs], reads, writes, dma=key, ninc=len(pairs))


class Ring:
    def __init__(self, items):
        self.items = items
        self.i = 0

    def next(self):
        x = self.items[self.i % len(self.items)]
        self.i += 1
        return x


def build(nlayers=2, nchunks=NCH):
    nc = bass.Bass("TRN2", target_bir_lowering=False)
    dt_in = lambda n, s, d=F32: nc.dram_tensor(n, s, d, kind="ExternalInput").ap()
    x_d = dt_in("x", [SEQ, D])
    pos_d = dt_in("pos", [1, SEQ], I32)
    w_in_d = dt_in("w_in", [D, 3072])
    w_out_d = dt_in("w_out", [D, D])
    w_qkv_d = dt_in("w_qkv", [D, 1280])
    w_o_d = dt_in("w_o", [D, D])
    w1_d = dt_in("w1", [2, D, 4096])
    w2_d = dt_in("w2", [2, 4096, D])
    pp_d = dt_in("pp", [128, 56])
    bc_d = dt_in("bc", [1, 528])
    bc2_d = dt_in("bc2", [5, D])
    cst_d = dt_in("cst", [128, 896])
    out_d = nc.dram_tensor("out", [SEQ, D], F32, kind="ExternalOutput").ap()

    es = ExitStack()
    with es:
        S = Sched(nc, es)
        sbt = lambda n, s, d: es.enter_context(nc.sbuf_tensor("sb_" + n, s, d))
        xs_t = sbt("xs", [128, TT, D], F32)
        hT_t = sbt("hT", [128, DC, T], BF16)
        xn_t = sbt("xn", [128, 2, D], BF16)
        KT0_t = sbt("KT0", [128, 4, SEQ], BF16)
        V0_t = sbt("V0", [128, SEQ // 128, 512], BF16)
        ar_t = sbt("arena", [128, 32, T], BF16)
        CS_t = sbt("CS", [128, 2, T], F32)
        F_t = sbt("Fp", [128, 6, 514], F32)
        W_t = sbt("wring", [128, NSLOT, 4096], BF16)
        tb_t = sbt("tb", [128, 2, D], F32)
        bcr_t = sbt("bcr", [128, 2, D], F32)
        KT1_t = sbt("KT1", [128, 2, 640], BF16)
        V1_t = sbt("V1", [128, 5, 256], BF16)
        junk_t = sbt("junk", [128, D], BF16)
        rtmp_t = sbt("rtmp", [128, 2, T], BF16)
        cst_t = sbt("cst", [128, 896], BF16)
        pp_t = sbt("pp", [128, 56], F32)
        bcs_t = sbt("bcs", [128, 528], F32)
        cf_t = sbt("cf", [128, 4], F32)
        sm_t = sbt("sm", [128, 8, 8], F32)
        lam_t = sbt("lam", [128, 8], F32)
        es_t = sbt("es", [128, 16], F32)
        carry_t = sbt("carry", [128, 4, 2], F32)
        posi_t = sbt("posi", [128, T], I32)
        ni_t = sbt("ni", [128, T], I32)
        ps_all = es.enter_context(nc.psum_tensor("psall", [128, 8, 512], F32))

        class _PS:
            def __getitem__(self, i):
                return ps_all[:, i, :]
        ps_t = _PS()

        R_xs = [Reg() for _ in range(TT)]
        R_hT = [Reg() for _ in range(TT)]
        R_xn = [Reg() for _ in range(2)]
        R_KT0 = [[Reg() for _ in range(NCH)] for _ in range(4)]
        R_V0 = [Reg() for _ in range(SEQ // 128)]
        R_ar = [Reg() for _ in range(32)]
        R_CS = Reg()
        R_F = [Reg() for _ in range(6)]
        R_W = [Reg() for _ in range(NSLOT)]
        R_tb = [Reg() for _ in range(2)]
        R_bcr = [Reg() for _ in range(2)]
        R_KT1 = [[Reg() for _ in range(5)] for _ in range(2)]
        R_V1 = [Reg() for _ in range(5)]
        R_junk = Reg()
        R_rtmp = [Reg(), Reg()]
        R_cst = Reg()
        R_pp = Reg()
        R_bcs = Reg()
        R_cf = Reg()
        R_sm = [Reg() for _ in range(8)]
        R_lam = Reg()
        R_es = Reg()
        R_carry = [Reg() for _ in range(4)]
        R_posi = Reg()
        R_ni = Reg()
        R_lt = Reg()
        R_ps = [Reg() for _ in range(8)]
        R_out = Reg()

        def xs(tt, a=0, b=D):
            return TV(xs_t[:, tt, a:b], [R_xs[tt]])

        def ar(i, a=0, b=T, p0=0, p1=128):
            return TV(ar_t[p0:p1, i, a:b], [R_ar[i]])

        QT0, MIX0, PT0, OSQ, QRAW0 = 0, 8, 16, 22, 23
        pt_ring = Ring([PT0 + i for i in range(6)] + [30, 31])
        qraw_ring = Ring([QRAW0, QRAW0 + 1, 27])
        rtmp_ring = Ring([0, 1])
        f_ring = Ring(list(range(6)))
        sm_ring = Ring(list(range(8)))
        tb_ring = Ring([0, 1])
        bcr_ring = Ring([0, 1])
        psA = Ring([0, 1, 2, 3])
        psAll = Ring(list(range(8)))

        def Ft(i, a=0, b=512, p0=0, p1=128):
            return TV(F_t[p0:p1, i, a:b], [R_F[i]])

        def psb(i, a=0, b=512, p0=0, p1=128):
            return TV(ps_all[p0:p1, i, a:b], [R_ps[i]])

        def ps2(tt):
            return TV(ps_all[:, 2 * tt:2 * tt + 2, :], [R_ps[2 * tt], R_ps[2 * tt + 1]])

        def ppc(c):
            return TV(pp_t[:, c:c + 1], [R_pp])

        def cfc(c):
            return TV(cf_t[:, c:c + 1], [R_cf])

        ident = TV(cst_t[:, 0:128], [R_cst])
        tri_ge = cst_t[:, 128:256]
        tri_gt = cst_t[:, 256:384]
        ones_b = TV(cst_t[:, 384:512], [R_cst])
        rmat = TV(cst_t[:, 512:640], [R_cst])
        Ctab = TV(CS_t[:, 0, :], [R_CS])
        Stab = TV(CS_t[:, 1, :], [R_CS])

        S.dma("pool", [(cst_t[:], cst_d)], [], [R_cst], "c0")
        S.dma("sp", [(pp_t[:], pp_d)], [], [R_pp], "c1")
        S.dma("sp", [(bcs_t[:], bc_d.partition_broadcast(128))], [], [R_bcs], "c2")
        S.memset("pool", TV(cf_t[:, 0:1], [R_cf]), 0.0)
        S.memset("pool", TV(cf_t[:, 1:2], [R_cf]), 1e-6)
        S.memset("pool", TV(cf_t[:, 2:3], [R_cf]), 1e-5)
        for cb in range(4):
            S.memset("pool", TV(carry_t[:, cb, :], [R_carry[cb]]), 0.0)
        bcs = lambda a, b: TV(bcs_t[:, a:b], [R_bcs])
        lt = lambda i: TV(F_t[:, i, 0:64], [R_F[i]])
        lamc = lambda c: TV(lam_t[:, c:c + 1], [R_lam])
        S.tt("dve", lt(0), bcs(0, 64), bcs(64, 128), ALU.mult)
        S.tt("dve", lt(1), bcs(128, 192), bcs(192, 256), ALU.mult)
        S.act(lt(0), lt(0), AF.Copy, accum=lamc(0))
        S.act(lt(1), lt(1), AF.Copy, accum=lamc(1))
        S.act(lamc(0), lamc(0), AF.Exp)
        S.act(lamc(1), lamc(1), AF.Exp)
        S.tt("dve", lamc(2), lamc(1), lamc(0), ALU.subtract)
        S.ts("dve", lamc(2), lamc(2), -0.2, None, ALU.add)
        S.ts("dve", lamc(3), ppc(44), 0.8, None, ALU.mult)
        neglam = lamc(2)
        wsub = lamc(3)
        S.act(TV(es_t[:], [R_es]), bcs(256, 272), AF.Exp)
        bvb = bcs(272, 528)

        def wslot(s):
            return W_t[:, s, :]

        def g_cols(src, c0):
            return lambda s: [(wslot(s).rearrange("p (c f) -> p c f", c=8), src[:, c0:c0 + 512].rearrange("(c p) f -> p c f", p=128))]

        def g_rows(src, r0):
            return lambda s: [(wslot(s).rearrange("p (c f) -> p c f", c=4), src[r0:r0 + 512, :].rearrange("(c p) f -> p c f", p=128))]

        def g_A(cb):
            def f(s):
                dst = wslot(s).rearrange("p (c j n) -> p c j n", c=8, j=4)
                src = w_in_d[:, 0:2048].rearrange("(c p) (j n) -> p c j n", p=128, j=4)[:, :, :, cb * 128:(cb + 1) * 128]
                return [(dst[:, :, j, :], src[:, :, j, :]) for j in range(4)]
            return f

        def g_kv1(s):
            dst = wslot(s).rearrange("p (c f) -> p c f", c=8)
            prs = []
            for part, c0 in ((0, 1024), (1, 1152)):
                d5 = dst[:, :, part * 256:(part + 1) * 256].rearrange("p c (h u d) -> p c h u d", h=2, u=2)
                src = w_qkv_d[:, c0:c0 + 128].rearrange("(c p) (h d) -> p c h d", p=128, h=2)
                for u in range(2):
                    for h in range(2):
                        prs.append((d5[:, :, h, u, :], src[:, :, h, :]))
            return prs

        def mlp_groups(l):
            return [g_cols(w1_d[l], g * 512) for g in range(8)] + [g_rows(w2_d[l], g * 512) for g in range(8)]

        chunk_groups = ([g_cols(w_in_d, 2048), g_cols(w_in_d, 2560)] + [g_A(cb) for cb in range(4)]
                        + [g_cols(w_out_d, 0), g_cols(w_out_d, 512)] + mlp_groups(0))
        if nlayers > 1:
            chunk_groups += ([g_kv1, g_cols(w_qkv_d, 0), g_cols(w_qkv_d, 512), g_cols(w_o_d, 0), g_cols(w_o_d, 512)]
                             + mlp_groups(1))
        all_groups = chunk_groups * nchunks
        wstate = {"issued": 0, "next": 0}

        NG = len(chunk_groups)
        scr_d = nc.dram_tensor("wscr", [NG, 128, 4096], BF16, kind="Internal").ap()
        R_scr = [Reg() for _ in range(NG)]

        def w_issue_upto(n):
            while wstate["issued"] < min(n, len(all_groups)):
                i = wstate["issued"]
                s = i % NSLOT
                if i < NG:
                    S.dma("pool", all_groups[i](s), [], [R_W[s]], ("wc", s))
                else:
                    S.dma("sp", [(W_t[:, s, :], scr_d[i % NG])], [R_scr[i % NG]], [R_W[s]], ("w", s))
                wstate["issued"] += 1

        def w_writeback(i):
            if i < NG and nchunks > 1:
                s = i % NSLOT
                S.dma("sp", [(scr_d[i], W_t[:, s, :])], [R_W[s]], [R_scr[i]], ("scr", s))

        def w_next():
            i = wstate["next"]
            w_issue_upto(i + NSLOT)
            wstate["next"] += 1
            w_writeback(i)
            return i % NSLOT

        def w_next2():
            i = wstate["next"]
            w_issue_upto(i + NSLOT)
            wstate["next"] += 2
            w_writeback(i)
            w_writeback(i + 1)
            return [i % NSLOT, (i + 1) % NSLOT]

        def Wc(s, dc, a, b):
            return TV(W_t[:, s, dc * 512 + a: dc * 512 + b], [R_W[s]])

        def Wr(s, kb, a, b):
            return TV(W_t[:, s, kb * 1024 + a: kb * 1024 + b], [R_W[s]])

        def hT(dc, a=0, b=T):
            regs = [R_hT[t] for t in range(a // 128, (b + 127) // 128)]
            return TV(hT_t[:, dc, a:b], regs)

        def small():
            i = sm_ring.next()
            f = lambda c: TV(sm_t[:, i, c:c + 1], [R_sm[i]])
            f.idx = i
            return f

        def sm_ap(sm, a, b):
            return sm_t[:, sm.idx, a:b]

        def rstd_from(sm, src_col, dst_col, scale, eps_col):
            S.act(sm(dst_col), sm(src_col), AF.Ln, bias=cfc(eps_col), scale=scale)
            S.act(sm(dst_col), sm(dst_col), AF.Exp, scale=-0.5)

        def norm_to_hT(gbase):
            sm = small()
            for tt in range(TT):
                S.act(TV(junk_t[:], []), xs(tt), AF.Square, accum=sm(tt))
                S.act(sm(4 + tt), sm(tt), AF.Ln, bias=cfc(1), scale=1.0 / D)
                S.act(sm(4 + tt), sm(4 + tt), AF.Exp, scale=-0.5)
            for tt in range(TT):
                xb = tt % 2
                xn = TV(xn_t[:, xb, :], [R_xn[xb]])
                S.act(xn, xs(tt), AF.Copy, scale=sm(4 + tt))
                pb = psAll.next()
                pbf = ps_all[:, pb, :].bitcast(BF16)
                for dc in range(DC):
                    S.tr(TV(pbf[:, dc * 128:(dc + 1) * 128], [R_ps[pb]]),
                         TV(xn_t[:, xb, dc * 128:(dc + 1) * 128], [R_xn[xb]]), ident)
                gain = TV(pp_t[:, gbase:gbase + 8].unsqueeze(2).broadcast_to([128, 8, 128]), [R_pp])
                S.tt("dve", TV(hT_t[:, :, tt * 128:(tt + 1) * 128], [R_hT[tt]]),
                     TV(pbf.rearrange("p (c t) -> p c t", c=8), [R_ps[pb]]), gain, ALU.mult)

        def bc_load(row):
            s = bcr_ring.next()
            S.dma("sp", [(bcr_t[:, s, :], bc2_d[row:row + 1, :].partition_broadcast(128))], [], [R_bcr[s]], ("bc", s))
            return lambda a, b: TV(bcr_t[:, s, a:b], [R_bcr[s]])

        def post_norm_all(gp, final=None):
            sm = small()
            smi = sm(0).regs
            for tt in range(TT):
                S.act(TV(junk_t[:, :].rearrange("p (a b) -> p a b", a=2), []), ps2(tt), AF.Square, accum=sm(tt))
                S.act(sm(4 + tt), sm(tt), AF.Ln, bias=cfc(1), scale=1.0 / D)
                S.act(sm(4 + tt), sm(4 + tt), AF.Exp, scale=-0.5)
            for tt in range(TT):
                ti = tb_ring.next()
                tb3 = TV(tb_t[:, ti, :].rearrange("p (a b) -> p a b", a=2), [R_tb[ti]])
                tbf = TV(tb_t[:, ti, :], [R_tb[ti]])
                S.stt("dve", tb3, ps2(tt), sm(4 + tt), TV(gp(0, 1024).ap.rearrange("p (a b) -> p a b", a=2), gp(0, 1024).regs),
                      ALU.mult, ALU.mult)
                eng = "pool" if tt % 2 == 0 else "dve"
                if final is None:
                    S.tt(eng, xs(tt), xs(tt), tbf, ALU.add)
                else:
                    S.tt(eng, tbf, xs(tt), tbf, ALU.add)
                    final(tt, tbf)

        rope_pending = []

        def rope_block(pb, dest, bias):
            qi = qraw_ring.next()
            qr = ar(qi)
            if bias is None:
                S.copy("act", qr, psb(pb))
            else:
                S.act(qr, psb(pb), AF.Identity, bias=bias)

            def part_b():
                S.mm(psb(pb), rmat, qr)
                f1 = f_ring.next()
                f2 = f_ring.next()
                S.tt("dve", Ft(f1), psb(pb), Stab, ALU.mult)
                S.tt("pool", Ft(f2), qr, Ctab, ALU.mult)
                S.tt("dve", dest, Ft(f2), Ft(f1), ALU.add)
            rope_pending.append(part_b)

        def rope_flush(keep=0):
            while len(rope_pending) > keep:
                rope_pending.pop(0)()

        def rope_tables(ci):
            S.dma("sp", [(posi_t[:], pos_d[0:1, ci * T:(ci + 1) * T].partition_broadcast(128))], [], [R_posi], "pos")
            posi = TV(posi_t[:], [R_posi])
            ni = TV(ni_t[:], [R_ni])
            fa, fb = f_ring.next(), f_ring.next()
            S.copy("dve", Ft(fa), posi)
            for which, tab in ((0, Ctab), (1, Stab)):
                S.ts("dve", Ft(fb), Ft(fa), ppc(55), (np.pi / 2 if which == 0 else 0.0), ALU.mult, ALU.add)
                S.ts("dve", ni, Ft(fb), float(1 / (2 * np.pi)), None, ALU.mult)
                fc = f_ring.next()
                S.copy("dve", Ft(fc), ni)
                S.stt("dve", Ft(fb), Ft(fc), -C1, Ft(fb), ALU.mult, ALU.add)
                S.stt("dve", Ft(fb), Ft(fc), -C2, Ft(fb), ALU.mult, ALU.add)
                S.ts("dve", Ft(fb), Ft(fb), -3.1415925, 3.1415925, ALU.max, ALU.min)
                S.act(tab, Ft(fb), AF.Sin)

        def mlp(l, gpre, gpost_row, after_tt=None, mid=None):
            norm_to_hT(gpre)
            for g in range(8):
                s = w_next()
                for j in range(4):
                    fb = g * 4 + j
                    pb = psAll.next()
                    for dc in range(DC):
                        S.mm(psb(pb), Wc(s, dc, j * 128, (j + 1) * 128), hT(dc), start=(dc == 0), stop=(dc == DC - 1))
                    ri = rtmp_ring.next()
                    rt = TV(rtmp_t[:, ri, :], [R_rtmp[ri]])
                    S.act(rt, psb(pb), AF.Relu)
                    S.tt("pool" if fb % 2 == 0 else "dve", ar(fb), rt, rt, ALU.mult)
                if g == 2 and mid is not None:
                    mid()
            gp = bc_load(gpost_row)
            for g in range(6):
                s = w_next()
                for tt in range(TT):
                    for fh in range(2):
                        for kb in range(4):
                            S.mm(psb(tt * 2 + fh), ar(g * 4 + kb, tt * 128, (tt + 1) * 128), Wr(s, kb, fh * 512, (fh + 1) * 512),
                                 start=(g == 0 and kb == 0), stop=False)
            s67 = dict(zip((6, 7), w_next2()))
            for tt in range(TT):
                for fh in range(2):
                    for g in (6, 7):
                        for kb in range(4):
                            S.mm(psb(tt * 2 + fh), ar(g * 4 + kb, tt * 128, (tt + 1) * 128), Wr(s67[g], kb, fh * 512, (fh + 1) * 512),
                                 start=False, stop=(g == 7 and kb == 3))
            post_norm_all(gp, final=after_tt)

        def out_proj(gpost_row, bias_rows=None):
            gp = bc_load(gpost_row)
            ss_ = w_next2()
            for tt in range(TT):
                for fh in range(2):
                    s = ss_[fh]
                    for kb in range(8):
                        S.mm(psb(tt * 2 + fh), ar(MIX0 + kb, tt * 128, (tt + 1) * 128), Wc(s, kb, 0, 512),
                             start=(kb == 0), stop=(kb == 7 and bias_rows is None))
                    if bias_rows is not None:
                        p_, brow = bias_rows[fh]
                        S.mm(psb(tt * 2 + fh), TV(cst_t[p_:p_ + 1, 384:512], [R_cst]), brow, start=False, stop=True)
            post_norm_all(gp)

        pending_fin = []

        def attn_l0(h, ci):
            nkt = 4 * (ci + 1)
            OT = [4, 5]
            RS = [6, 7]
            pts = {}

            def q0_of(kt):
                j = kt - 4 * ci
                return 128 * j if j > 0 else 0

            def qk(kt):
                q0 = q0_of(kt)
                for c in range(2):
                    sb_ = 2 * (kt % 2) + c
                    kt_tv = TV(KT0_t[c * 64:(c + 1) * 64, h, kt * 128:(kt + 1) * 128], [R_KT0[h][kt // 4]])
                    S.mm(psb(sb_, q0, 512), kt_tv, ar(QT0 + h, q0, 512, c * 64, (c + 1) * 64))

            def ex(kt):
                q0 = q0_of(kt)
                for c in range(2):
                    sb_ = 2 * (kt % 2) + c
                    pi = pt_ring.next()
                    S.act(ar(pi, q0, 512), psb(sb_, q0, 512), AF.Exp, scale=0.125)
                    if kt - 4 * ci >= 0:
                        S.tt("pool", ar(pi, q0, q0 + 128), ar(pi, q0, q0 + 128), TV(tri_ge, [R_cst]), ALU.mult)
                    pts[(kt, c)] = pi

            def av(kt):
                q0 = q0_of(kt)
                for c in range(2):
                    pi = pts.pop((kt, c))
                    S.mm(psb(OT[c], q0, 512), TV(V0_t[:, kt, h * 128:(h + 1) * 128], [R_V0[kt]]), ar(pi, q0, 512),
                         start=(kt == 0), stop=(kt == nkt - 1))
                    S.mm(psb(RS[c], q0, 512), ones_b, ar(pi, q0, 512), start=(kt == 0), stop=(kt == nkt - 1))

            qk(0)
            ex(0)
            qk(1)
            ex(1)
            for n in range(nkt):
                S.batch_begin("pe")
                if n + 2 < nkt:
                    qk(n + 2)
                av(n)
                S.batch_end("pe")
                if n + 2 < nkt:
                    ex(n + 2)
                if n == 1 and pending_fin:
                    pending_fin.pop(0)()
            f0, f1, f2, f3 = f_ring.next(), f_ring.next(), f_ring.next(), f_ring.next()
            S.act(Ft(f0), psb(RS[0]), AF.Ln)
            S.act(Ft(f0), Ft(f0), AF.Exp, scale=-1.0)
            S.tt("dve", Ft(f1), psb(OT[0]), Ft(f0), ALU.mult)
            S.act(Ft(f2), psb(RS[1]), AF.Ln)
            S.act(Ft(f2), Ft(f2), AF.Exp, scale=-1.0)
            S.tt("dve", Ft(f3), psb(OT[1]), Ft(f2), ALU.mult)
            S.stt("dve", Ft(f1), Ft(f3), neglam, Ft(f1), ALU.mult, ALU.add)
            S.tt("pool", ar(OSQ), Ft(f1), Ft(f1), ALU.mult)

            def fin2():
                sb_ = psA.next()
                S.mm(psb(sb_), ones_b, ar(OSQ))
                S.act(Ft(f0), psb(sb_), AF.Ln, bias=cfc(2), scale=1.0 / 128)
                S.act(Ft(f0), Ft(f0), AF.Exp, scale=-0.5)
                S.stt("dve", ar(MIX0 + 4 + h), Ft(f1), wsub, Ft(f0), ALU.mult, ALU.mult)
            pending_fin.append(fin2)

        def mixer0(ci):
            norm_to_hT(0)
            s = w_next()
            for h in range(4):
                pb = psAll.next()
                for dc in range(DC):
                    S.mm(psb(pb), Wc(s, dc, h * 128, (h + 1) * 128), hT(dc), start=(dc == 0), stop=(dc == DC - 1))
                rope_block(pb, TV(KT0_t[:, h, ci * T:(ci + 1) * T], [R_KT0[h][ci]]), None)
                rope_flush(1)
            s = w_next()
            for tt in range(TT):
                pb = psAll.next()
                for dc in range(DC):
                    S.mm(psb(pb), hT(dc, tt * 128, (tt + 1) * 128), Wc(s, dc, 0, 512), start=(dc == 0), stop=(dc == DC - 1))
                S.copy("act" if tt % 2 else "dve", TV(V0_t[:, ci * 4 + tt, :], [R_V0[ci * 4 + tt]]), psb(pb))
                rope_flush(0)
            for cb in range(4):
                s = w_next()
                pbs = [psAll.next() for _ in range(4)]
                p_gb, p_gc, p_xc, p_q = pbs
                for j in (3, 0, 1, 2):
                    for dc in range(DC):
                        S.mm(psb(pbs[j]), Wc(s, dc, j * 128, (j + 1) * 128), hT(dc), start=(dc == 0), stop=(dc == DC - 1))
                    if j == 3:
                        rope_block(p_q, ar(QT0 + cb), None)
                rope_flush(0)
                fx, fg, fu, fa = f_ring.next(), f_ring.next(), f_ring.next(), f_ring.next()
                S.copy("act", Ft(fx), psb(p_xc))
                S.copy("act", Ft(fg), psb(p_gc))
                cr = TV(carry_t[:, cb, :], [R_carry[cb]])
                S.copy("pool", Ft(fu, 0, 2), cr)
                S.tt("pool", Ft(fu, 2, 514), Ft(fg), Ft(fx), ALU.mult)
                S.copy("pool", cr, Ft(fu, 512, 514))
                S.act(Ft(fa), Ft(fu, 2, 514), AF.Copy, scale=ppc(32 + 8 + cb))
                S.stt("dve", Ft(fa), Ft(fu, 1, 513), ppc(32 + 4 + cb), Ft(fa), ALU.mult, ALU.add)
                S.stt("dve", Ft(fa), Ft(fu, 0, 512), ppc(32 + cb), Ft(fa), ALU.mult, ALU.add)
                S.tt("dve", ar(MIX0 + cb), psb(p_gb), Ft(fa), ALU.mult)
            for h in range(4):
                attn_l0(h, ci)
            while pending_fin:
                pending_fin.pop(0)()
            out_proj(0)

        es3 = es_t[:, :].rearrange("p (i two) -> p i two", two=2)

        def attn_l1(ci):
            st_ring = Ring([0, 1])
            pair_of = {}
            def front(qb, kvh, par, banks=None):
                gblk = ci * TT + qb
                p0, p1 = par * 64, (par + 1) * 64
                rq = TV(ar_t[p0:p1, QT0 + kvh * 4:QT0 + kvh * 4 + 4, qb * 128:(qb + 1) * 128],
                        [R_ar[QT0 + kvh * 4 + i] for i in range(4)])
                kblocks = ([(qb, 768)] if gblk > 0 else []) + [(qb + 1, 640)]
                pts = []
                for n_k, (kb_, mcol) in enumerate(kblocks):
                    sb_ = st_ring.next() if banks is None else banks[n_k]
                    st3 = TV(ps_all[:, sb_, :].rearrange("p (i q) -> p i q", i=4), [R_ps[sb_]])
                    S.mm(st3, TV(KT1_t[p0:p1, kvh, kb_ * 128:(kb_ + 1) * 128], [R_KT1[kvh][kb_]]), rq, start=True, stop=False)
                    S.mm(st3, ident, TV(cst_t[:, mcol:mcol + 128].unsqueeze(1).broadcast_to([128, 4, 128]), [R_cst]),
                         start=False, stop=True)
                    pi = pt_ring.next()
                    S.act(ar(pi), psb(sb_), AF.Exp, scale=0.125)
                    pts.append((kb_, pi))
                return pts

            def back(qb, kvh, par, pts):
                p0, p1 = par * 64, (par + 1) * 64
                ob, rb = pair_of[(qb, kvh, par)]
                for n_, (kb_, pi) in enumerate(pts):
                    S.mm(psb(ob), TV(V1_t[:, kb_, kvh * 128:(kvh + 1) * 128], [R_V1[kb_]]), ar(pi),
                         start=(n_ == 0), stop=(n_ == len(pts) - 1))
                for n_, (kb_, pi) in enumerate(pts):
                    S.mm(psb(rb), ones_b, ar(pi), start=(n_ == 0), stop=False)
                ep_, eblk = ESROW[(kvh, par)]
                S.mm(psb(rb), TV(cst_t[ep_:ep_ + 1, 384:512], [R_cst]), TV(ar_t[ep_:ep_ + 1, eblk, :], [R_ar[eblk]]),
                     start=False, stop=True)
                fd = f_ring.next()
                den = TV(F_t[p0:p1, fd, 0:512].rearrange("p (i q) -> p i q", i=4), [R_F[fd]])
                S.act(Ft(fd, 0, 512, p0, p1), psb(rb, 0, 512, p0, p1), AF.Ln)

                def tail():
                    S.act(Ft(fd, 0, 512, p0, p1), Ft(fd, 0, 512, p0, p1), AF.Exp, scale=-1.0)
                    S.tt("dve", TV(ar_t[p0:p1, MIX0 + kvh * 4:MIX0 + kvh * 4 + 4, qb * 128:(qb + 1) * 128],
                                   [R_ar[MIX0 + kvh * 4 + i] for i in range(4)]),
                         TV(ps_all[p0:p1, ob, :].rearrange("p (i q) -> p i q", i=4), [R_ps[ob]]), den, ALU.mult)
                return tail

            its = [(qb, kvh, par) for kvh in range(2) for qb in range(TT) for par in range(2)]
            LAG = 3
            fr = {}
            for i in range(min(LAG, len(its))):
                fr[i] = front(*its[i], banks=(2 * i, 2 * i + 1))
            for i, it_ in enumerate(its):
                pair_of[it_] = (2 + 2 * (i % 3), 3 + 2 * (i % 3))
            def main():
                prev_tail = None
                for i in range(len(its)):
                    S.batch_begin("pe")
                    if i + LAG < len(its):
                        fr[i + LAG] = front(*its[i + LAG])
                    tl = back(*its[i], fr.pop(i))
                    S.batch_end("pe")
                    if prev_tail is not None:
                        prev_tail()
                    prev_tail = tl
                prev_tail()
            return main

        ESROW = {(0, 0): (0, 28), (0, 1): (32, 28), (1, 0): (64, 28), (1, 1): (0, 29)}
        BOROW = [(32, 29), (64, 29)]

        def mixer1(ci):
            bo = bc_load(4)
            for (kvh, par), (p_, blk) in ESROW.items():
                S.copy("dve", TV(ar_t[p_:p_ + 1, blk, :].rearrange("p (i q) -> p i q", i=4), [R_ar[blk]]),
                       TV(es3[p_:p_ + 1, kvh * 4:(kvh + 1) * 4, par].unsqueeze(2).broadcast_to([1, 4, 128]), [R_es]))
            bo_rows = []
            for fh, (p_, blk) in enumerate(BOROW):
                src = bo(fh * 512, (fh + 1) * 512)
                S.copy("dve", TV(ar_t[p_:p_ + 1, blk, :], [R_ar[blk]]), TV(src.ap[p_:p_ + 1, :], src.regs))
                bo_rows.append((p_, TV(ar_t[p_:p_ + 1, blk, :], [R_ar[blk]])))
            norm_to_hT(16)
            s = w_next()
            skv = s
            for kvh in range(2):
                pb = psAll.next()
                for dc in range(DC):
                    S.mm(psb(pb), Wc(s, dc, kvh * 128, (kvh + 1) * 128), hT(dc), start=(dc == 0), stop=(dc == DC - 1))
                rope_block(pb, TV(KT1_t[:, kvh, 128:640], [R_KT1[kvh][i] for i in range(1, 5)]), ppc(53 + kvh))
                rope_flush(1)
            for tt in range(TT):
                pb = psAll.next()
                for dc in range(DC):
                    S.mm(psb(pb, 0, 256), hT(dc, tt * 128, (tt + 1) * 128), Wc(skv, dc, 256, 512), start=(dc == 0), stop=(dc == DC - 1))
                S.tt("dve", TV(V1_t[:, 1 + tt, :], [R_V1[1 + tt]]), psb(pb, 0, 256), bvb, ALU.add)
                rope_flush(0)
            for qh in range(2):
                s = w_next()
                for j in range(4):
                    blk = qh * 4 + j
                    pb = psAll.next()
                    for dc in range(DC):
                        S.mm(psb(pb), Wc(s, dc, j * 128, (j + 1) * 128), hT(dc), start=(dc == 0), stop=(dc == DC - 1))
                    rope_block(pb, ar(QT0 + blk), ppc(45 + blk))
                    rope_flush(1)
                    if blk == 5:
                        attn_main = attn_l1(ci)
            rope_flush(0)
            attn_main()
            for kvh in range(2):
                S.copy("pool", TV(KT1_t[:, kvh, 0:128], [R_KT1[kvh][0]]), TV(KT1_t[:, kvh, 512:640], [R_KT1[kvh][4]]))
            S.copy("pool", TV(V1_t[:, 0, :], [R_V1[0]]), TV(V1_t[:, 4, :], [R_V1[4]]))
            out_proj(1, bias_rows=bo_rows)

        stores = []

        def load_x(ci, tt):
            r0 = ci * T + tt * 128
            S.dma("sp", [(xs_t[:, tt, :], x_d[r0:r0 + 128, :])], [], [R_xs[tt]], ("x", tt))

        for tt in range(TT):
            load_x(0, tt)
        for ci in range(nchunks):
            def after_tt(tt, src, ci=ci):
                r0 = ci * T + tt * 128
                stores.append(S.dma("sp", [(out_d[r0:r0 + 128, :], src.ap)], list(src.regs), [], ("y", tt)))
                if ci + 1 < nchunks:
                    load_x(ci + 1, tt)
            if ci == 0:
                rope_tables(0)
            mixer0(ci)
            if nlayers > 1:
                mlp(0, 8, 2)
                mixer1(ci)
                mlp(1, 24, 3, after_tt, mid=((lambda ci=ci: rope_tables(ci + 1)) if ci + 1 < nchunks else None))
            else:
                mlp(0, 8, 2, after_tt)
        S.emit("sp", stores[-TT:])
    return nc


def _host_consts():
    cst = np.zeros((128, 896), np.float32)
    k = np.arange(128)[:, None]
    q = np.arange(128)[None, :]
    cst[:, 0:128] = (k == q)
    cst[:, 128:256] = (q >= k)
    cst[:, 256:384] = (k > q)
    cst[:, 384:512] = 1.0
    rm = np.zeros((128, 128), np.float32)
    for m in range(128):
        mm_ = m % 64
        if mm_ < 8:
            rm[m + 8, m] = -1.0
        elif mm_ < 16:
            rm[m - 8, m] = 1.0
    cst[:, 512:640] = rm
    cst[:, 640:768] = np.where(q >= k, 0.0, -30000.0)
    cst[:, 768:896] = np.where(k > q, 0.0, -30000.0)
    return cst


def _prep(inputs, nlayers=2):
    f = lambda a: np.ascontiguousarray(np.asarray(a, dtype=np.float32))
    pm = lambda v: np.asarray(v, np.float32).reshape(8, 128).T
    pp = np.zeros((128, 56), np.float32)
    pp[:, 0:8] = pm(inputs["norm_pre_mix"][0])
    pp[:, 8:16] = pm(inputs["norm_pre_mlp"][0])
    pp[:, 16:24] = pm(inputs["norm_pre_mix"][1])
    pp[:, 24:32] = pm(inputs["norm_pre_mlp"][1])
    cw = np.asarray(inputs["even_conv_w"][0], np.float32)
    for i in range(3):
        pp[:, 32 + i * 4:32 + i * 4 + 4] = cw[i].reshape(4, 128).T
    pp[:, 44] = np.asarray(inputs["even_subln_w"][0], np.float32)
    bq = np.asarray(inputs["odd_b_qkv"][0], np.float32)
    pp[:, 45:53] = pm(bq[0:1024])
    pp[:, 53] = np.tile(bq[1024:1088], 2)
    pp[:, 54] = np.tile(bq[1088:1152], 2)
    invf = (np.float32(500000.0) ** (-(np.arange(0, 16, 2, dtype=np.float32)) / np.float32(16))).astype(np.float32)
    col = np.zeros(128, np.float32)
    for p in range(128):
        if p % 64 < 16:
            col[p] = invf[p % 8]
    pp[:, 55] = col
    bc = np.zeros((1, 528), np.float32)
    bc[0, 0:64] = inputs["even_lambda_q1"][0]
    bc[0, 64:128] = inputs["even_lambda_k1"][0]
    bc[0, 128:192] = inputs["even_lambda_q2"][0]
    bc[0, 192:256] = inputs["even_lambda_k2"][0]
    bc[0, 256:272] = inputs["odd_sinks"][0]
    bv = bq[1152:1280]
    bc[0, 272:528] = np.concatenate([bv[0:64], bv[0:64], bv[64:128], bv[64:128]])
    bc2 = np.stack([np.asarray(inputs["norm_post_mix"][0], np.float32), np.asarray(inputs["norm_post_mix"][1], np.float32),
                    np.asarray(inputs["norm_post_mlp"][0], np.float32), np.asarray(inputs["norm_post_mlp"][1], np.float32),
                    np.asarray(inputs["odd_b_o"][0], np.float32)])
    shared = {
        "w_in": f(inputs["even_w_in"][0]), "w_out": f(inputs["even_w_out"][0]),
        "w_qkv": f(inputs["odd_w_qkv"][0]), "w_o": f(inputs["odd_w_o"][0]),
        "w1": f(inputs["mlp_w1"]), "w2": f(inputs["mlp_w2"]),
        "pp": pp, "bc": bc, "bc2": f(bc2), "cst": _host_consts(),
    }
    x = np.asarray(inputs["x"], np.float32)
    pos = np.asarray(inputs["positions"], np.int32)
    in_maps = []
    for b in range(8):
        m = dict(shared)
        m["x"] = np.ascontiguousarray(x[b])
        m["pos"] = np.ascontiguousarray(pos[b][None, :])
        in_maps.append(m)
    return in_maps


def kernel(**inputs):
    in_maps = _prep(inputs)
    nc = build()
    res = run_bass_kernel_spmd(nc, in_maps, core_ids=list(range(8)))
    return np.stack([np.asarray(r["out"], np.float32) for r in res.results], axis=0)
```

```python
import numpy as np
import concourse.bass as bass
import concourse.mybir as mybir
from concourse.bass_utils import run_bass_kernel_spmd
from contextlib import ExitStack

F32 = mybir.dt.float32
BF16 = mybir.dt.bfloat16
I32 = mybir.dt.int32
ALU = mybir.AluOpType
AF = mybir.ActivationFunctionType

SEQ = 4096
D = 1024
T = 512
NCH = SEQ // T
TT = T // 128
DC = D // 128
NSLOT = 4
C1 = 6.28125
C2 = float(2 * np.pi - 6.28125)


class Reg:
    __slots__ = ("lw", "rd")

    def __init__(self):
        self.lw = None
        self.rd = []


class TV:
    __slots__ = ("ap", "regs")

    def __init__(self, ap, regs):
        self.ap = ap
        self.regs = tuple(regs)


class Op:
    __slots__ = ("eng", "fn", "stream", "deps", "mark", "milestone", "ninc")


def _ap(x):
    return x.ap if isinstance(x, TV) else x


def _rg(*xs):
    out = []
    for x in xs:
        if isinstance(x, TV):
            out.extend(x.regs)
    return out


class Sched:
    def __init__(self, nc, es):
        self.nc = nc
        self.es = es
        self.eng = {"pe": nc.tensor, "act": nc.scalar, "dve": nc.vector, "pool": nc.gpsimd, "sp": nc.sync}
        self.ops = {k: [] for k in self.eng}
        self.sems = {}
        self.nops = 0
        self._batch = {}

    def batch_begin(self, eng):
        self._batch[eng] = None

    def batch_end(self, eng):
        del self._batch[eng]

    def sem(self, key):
        if key not in self.sems:
            self.sems[key] = self.es.enter_context(self.nc.semaphore("s%d" % len(self.sems)))
        return self.sems[key]

    def op(self, eng, fn, reads=(), writes=(), dma=None, ninc=1):
        o = Op()
        o.eng = eng
        o.fn = fn
        o.stream = ("dma", dma) if dma is not None else eng
        o.mark = None
        o.milestone = dma is not None
        o.ninc = ninc
        self.nops += 1
        deps = set()
        for r in reads:
            if r.lw is not None:
                deps.add(r.lw)
        for w in writes:
            if w.lw is not None:
                deps.add(w.lw)
            deps.update(w.rd)
        deps.discard(o)
        o.deps = [d for d in deps if not (d.stream == "pe" and eng == "pe")]
        for r in reads:
            r.rd.append(o)
        for w in writes:
            w.lw = o
            w.rd = []
        if eng in self._batch:
            first = self._batch[eng]
            if first is None:
                self._batch[eng] = o
            else:
                first.deps.extend(o.deps)
                o.deps = []
        self.ops[eng].append(o)
        return o

    def emit(self, final_eng, final_ops):
        for lst in self.ops.values():
            for o in lst:
                for d in o.deps:
                    d.milestone = True
        cnt = {}
        for lst in self.ops.values():
            for o in lst:
                if o.milestone:
                    inc = 16 * o.ninc if isinstance(o.stream, tuple) else 1
                    cnt[o.stream] = cnt.get(o.stream, 0) + inc
                    o.mark = cnt[o.stream]
        for e, lst in self.ops.items():
            E = self.eng[e]
            known = {}
            for o in lst:
                need = {}
                for d in o.deps:
                    if d.mark > need.get(d.stream, 0):
                        need[d.stream] = d.mark
                for s, v in need.items():
                    if known.get(s, 0) >= v:
                        continue
                    known[s] = v
                    E.wait_ge(self.sem(s), v)
                ins = o.fn(E)
                if o.milestone:
                    if isinstance(o.stream, tuple):
                        if not isinstance(ins, (list, tuple)):
                            ins = [ins]
                        assert len(ins) == o.ninc
                        for i_ in ins:
                            i_.then_inc(self.sem(o.stream), 16)
                    else:
                        ins.then_inc(self.sem(o.stream), 1)
            if e == final_eng:
                need = {}
                for d in final_ops:
                    if d.mark > need.get(d.stream, 0):
                        need[d.stream] = d.mark
                for s, v in need.items():
                    E.wait_ge(self.sem(s), v)

    def act(self, out, in_, func, bias=None, scale=1.0, accum=None):
        kw = {}
        if bias is not None:
            kw["bias"] = _ap(bias)
        if accum is not None:
            kw["accum_out"] = _ap(accum)
        o_, i_, s_ = _ap(out), _ap(in_), _ap(scale)
        return self.op("act", lambda e: e.activation(out=o_, in_=i_, func=func, scale=s_, **kw),
                       _rg(in_, bias, scale), _rg(out, accum))

    def ts(self, eng, out, in0, s1, s2, op0, op1=None):
        o_, i_, a_, b_ = _ap(out), _ap(in0), _ap(s1), _ap(s2)
        if op1 is None:
            f = lambda e: e.tensor_scalar(out=o_, in0=i_, scalar1=a_, scalar2=None, op0=op0)
        else:
            f = lambda e: e.tensor_scalar(out=o_, in0=i_, scalar1=a_, scalar2=b_, op0=op0, op1=op1)
        return self.op(eng, f, _rg(in0, s1, s2), _rg(out))

    def tt(self, eng, out, in0, in1, op):
        o_, a_, b_ = _ap(out), _ap(in0), _ap(in1)
        return self.op(eng, lambda e: e.tensor_tensor(out=o_, in0=a_, in1=b_, op=op), _rg(in0, in1), _rg(out))

    def stt(self, eng, out, in0, scalar, in1, op0, op1):
        o_, a_, s_, b_ = _ap(out), _ap(in0), _ap(scalar), _ap(in1)
        return self.op(eng, lambda e: e.scalar_tensor_tensor(out=o_, in0=a_, scalar=s_, in1=b_, op0=op0, op1=op1),
                       _rg(in0, scalar, in1), _rg(out))

    def copy(self, eng, out, in_):
        o_, i_ = _ap(out), _ap(in_)
        if eng == "act":
            return self.op("act", lambda e: e.activation(out=o_, in_=i_, func=AF.Copy), _rg(in_), _rg(out))
        return self.op(eng, lambda e: e.tensor_copy(out=o_, in_=i_), _rg(in_), _rg(out))

    def recip(self, out, in_):
        o_, i_ = _ap(out), _ap(in_)
        return self.op("dve", lambda e: e.reciprocal(out=o_, in_=i_), _rg(in_), _rg(out))

    def memset(self, eng, out, val):
        o_ = _ap(out)
        return self.op(eng, lambda e: e.memset(o_, val), (), _rg(out))

    def mm(self, out, lhsT, rhs, start=True, stop=True):
        o_, l_, r_ = _ap(out), _ap(lhsT), _ap(rhs)
        return self.op("pe", lambda e: e.matmul(o_, lhsT=l_, rhs=r_, start=start, stop=stop), _rg(lhsT, rhs), _rg(out))

    def tr(self, out, in_, ident):
        o_, i_, d_ = _ap(out), _ap(in_), _ap(ident)
        return self.op("pe", lambda e: e.transpose(out=o_, in_=i_, identity=d_), _rg(in_, ident), _rg(out))

    def dma(self, q, outs_ins, reads, writes, key):
        pairs = [(_ap(a), _ap(b)) for a, b in outs_ins]
        return self.op(q, lambda e: [e.dma_start(out=a, in_=b) for a, b in pairs], reads, writes, dma=key, ninc=len(pairs))


class Ring:
    def __init__(self, items):
        self.items = items
        self.i = 0

    def next(self):
        x = self.items[self.i % len(self.items)]
        self.i += 1
        return x


def build(nlayers=2, nchunks=NCH):
    nc = bass.Bass("TRN2", target_bir_lowering=False)
    dt_in = lambda n, s, d=F32: nc.dram_tensor(n, s, d, kind="ExternalInput").ap()
    x_d = dt_in("x", [SEQ, D])
    pos_d = dt_in("pos", [1, SEQ], I32)
    w_in_d = dt_in("w_in", [D, 3072])
    w_out_d = dt_in("w_out", [D, D])
    w_qkv_d = dt_in("w_qkv", [D, 1280])
    w_o_d = dt_in("w_o", [D, D])
    w1_d = dt_in("w1", [2, D, 4096])
    w2_d = dt_in("w2", [2, 4096, D])
    pp_d = dt_in("pp", [128, 56])
    bc_d = dt_in("bc", [1, 528])
    bc2_d = dt_in("bc2", [5, D])
    cst_d = dt_in("cst", [128, 896])
    out_d = nc.dram_tensor("out", [SEQ, D], F32, kind="ExternalOutput").ap()

    es = ExitStack()
    with es:
        S = Sched(nc, es)
        sbt = lambda n, s, d: es.enter_context(nc.sbuf_tensor("sb_" + n, s, d))
        xs_t = sbt("xs", [128, TT, D], F32)
        hT_t = sbt("hT", [128, DC, T], BF16)
        xn_t = sbt("xn", [128, 2, D], BF16)
        KT0_t = sbt("KT0", [128, 4, SEQ], BF16)
        V0_t = sbt("V0", [128, SEQ // 128, 512], BF16)
        ar_t = sbt("arena", [128, 32, T], BF16)
        CS_t = sbt("CS", [128, 2, T], F32)
        F_t = sbt("Fp", [128, 6, 514], F32)
        W_t = sbt("wring", [128, NSLOT, 4096], BF16)
        tb_t = sbt("tb", [128, 2, D], F32)
        bcr_t = sbt("bcr", [128, 2, D], F32)
        KT1_t = sbt("KT1", [128, 2, 640], BF16)
        V1_t = sbt("V1", [128, 5, 256], BF16)
        junk_t = sbt("junk", [128, D], BF16)
        rtmp_t = sbt("rtmp", [128, 2, T], BF16)
        cst_t = sbt("cst", [128, 896], BF16)
        pp_t = sbt("pp", [128, 56], F32)
        bcs_t = sbt("bcs", [128, 528], F32)
        cf_t = sbt("cf", [128, 4], F32)
        sm_t = sbt("sm", [128, 8, 8], F32)
        lam_t = sbt("lam", [128, 8], F32)
        es_t = sbt("es", [128, 16], F32)
        carry_t = sbt("carry", [128, 4, 2], F32)
        posi_t = sbt("posi", [128, T], I32)
        ni_t = sbt("ni", [128, T], I32)
        ps_all = es.enter_context(nc.psum_tensor("psall", [128, 8, 512], F32))

        class _PS:
            def __getitem__(self, i):
                return ps_all[:, i, :]
        ps_t = _PS()

        R_xs = [Reg() for _ in range(TT)]
        R_hT = [Reg() for _ in range(TT)]
        R_xn = [Reg() for _ in range(2)]
        R_KT0 = [[Reg() for _ in range(NCH)] for _ in range(4)]
        R_V0 = [Reg() for _ in range(SEQ // 128)]
        R_ar = [Reg() for _ in range(32)]
        R_CS = Reg()
        R_F = [Reg() for _ in range(6)]
        R_W = [Reg() for _ in range(NSLOT)]
        R_tb = [Reg() for _ in range(2)]
        R_bcr = [Reg() for _ in range(2)]
        R_KT1 = [[Reg() for _ in range(5)] for _ in range(2)]
        R_V1 = [Reg() for _ in range(5)]
        R_junk = Reg()
        R_rtmp = [Reg(), Reg()]
        R_cst = Reg()
        R_pp = Reg()
        R_bcs = Reg()
        R_cf = Reg()
        R_sm = [Reg() for _ in range(8)]
        R_lam = Reg()
        R_es = Reg()
        R_carry = [Reg() for _ in range(4)]
        R_posi = Reg()
        R_ni = Reg()
        R_lt = Reg()
        R_ps = [Reg() for _ in range(8)]
        R_out = Reg()

        def xs(tt, a=0, b=D):
            return TV(xs_t[:, tt, a:b], [R_xs[tt]])

        def ar(i, a=0, b=T, p0=0, p1=128):
            return TV(ar_t[p0:p1, i, a:b], [R_ar[i]])

        QT0, MIX0, PT0, OSQ, QRAW0 = 0, 8, 16, 22, 23
        pt_ring = Ring([PT0 + i for i in range(6)] + [30, 31])
        qraw_ring = Ring([QRAW0, QRAW0 + 1, 27])
        rtmp_ring = Ring([0, 1])
        f_ring = Ring(list(range(6)))
        sm_ring = Ring(list(range(8)))
        tb_ring = Ring([0, 1])
        bcr_ring = Ring([0, 1])
        psA = Ring([0, 1, 2, 3])
        psAll = Ring(list(range(8)))

        def Ft(i, a=0, b=512, p0=0, p1=128):
            return TV(F_t[p0:p1, i, a:b], [R_F[i]])

        def psb(i, a=0, b=512, p0=0, p1=128):
            return TV(ps_all[p0:p1, i, a:b], [R_ps[i]])

        def ps2(tt):
            return TV(ps_all[:, 2 * tt:2 * tt + 2, :], [R_ps[2 * tt], R_ps[2 * tt + 1]])

        def ppc(c):
            return TV(pp_t[:, c:c + 1], [R_pp])

        def cfc(c):
            return TV(cf_t[:, c:c + 1], [R_cf])

        ident = TV(cst_t[:, 0:128], [R_cst])
        tri_ge = cst_t[:, 128:256]
        tri_gt = cst_t[:, 256:384]
        ones_b = TV(cst_t[:, 384:512], [R_cst])
        rmat = TV(cst_t[:, 512:640], [R_cst])
        Ctab = TV(CS_t[:, 0, :], [R_CS])
        Stab = TV(CS_t[:, 1, :], [R_CS])

        S.dma("pool", [(cst_t[:], cst_d)], [], [R_cst], "c0")
        S.dma("sp", [(pp_t[:], pp_d)], [], [R_pp], "c1")
        S.dma("sp", [(bcs_t[:], bc_d.partition_broadcast(128))], [], [R_bcs], "c2")
        S.memset("pool", TV(cf_t[:, 0:1], [R_cf]), 0.0)
        S.memset("pool", TV(cf_t[:, 1:2], [R_cf]), 1e-6)
        S.memset("pool", TV(cf_t[:, 2:3], [R_cf]), 1e-5)
        for cb in range(4):
            S.memset("pool", TV(carry_t[:, cb, :], [R_carry[cb]]), 0.0)
        bcs = lambda a, b: TV(bcs_t[:, a:b], [R_bcs])
        lt = lambda i: TV(F_t[:, i, 0:64], [R_F[i]])
        lamc = lambda c: TV(lam_t[:, c:c + 1], [R_lam])
        S.tt("dve", lt(0), bcs(0, 64), bcs(64, 128), ALU.mult)
        S.tt("dve", lt(1), bcs(128, 192), bcs(192, 256), ALU.mult)
        S.act(lt(0), lt(0), AF.Copy, accum=lamc(0))
        S.act(lt(1), lt(1), AF.Copy, accum=lamc(1))
        S.act(lamc(0), lamc(0), AF.Exp)
        S.act(lamc(1), lamc(1), AF.Exp)
        S.tt("dve", lamc(2), lamc(1), lamc(0), ALU.subtract)
        S.ts("dve", lamc(2), lamc(2), -0.2, None, ALU.add)
        S.ts("dve", lamc(3), ppc(44), 0.8, None, ALU.mult)
        neglam = lamc(2)
        wsub = lamc(3)
        S.act(TV(es_t[:], [R_es]), bcs(256, 272), AF.Exp)
        bvb = bcs(272, 528)

        def wslot(s):
            return W_t[:, s, :]

        def g_cols(src, c0):
            return lambda s: [(wslot(s).rearrange("p (c f) -> p c f", c=8), src[:, c0:c0 + 512].rearrange("(c p) f -> p c f", p=128))]

        def g_rows(src, r0):
            return lambda s: [(wslot(s).rearrange("p (c f) -> p c f", c=4), src[r0:r0 + 512, :].rearrange("(c p) f -> p c f", p=128))]

        def g_A(cb):
            def f(s):
                dst = wslot(s).rearrange("p (c j n) -> p c j n", c=8, j=4)
                src = w_in_d[:, 0:2048].rearrange("(c p) (j n) -> p c j n", p=128, j=4)[:, :, :, cb * 128:(cb + 1) * 128]
                return [(dst[:, :, j, :], src[:, :, j, :]) for j in range(4)]
            return f

        def g_kv1(s):
            dst = wslot(s).rearrange("p (c f) -> p c f", c=8)
            prs = []
            for part, c0 in ((0, 1024), (1, 1152)):
                d5 = dst[:, :, part * 256:(part + 1) * 256].rearrange("p c (h u d) -> p c h u d", h=2, u=2)
                src = w_qkv_d[:, c0:c0 + 128].rearrange("(c p) (h d) -> p c h d", p=128, h=2)
                for u in range(2):
                    for h in range(2):
                        prs.append((d5[:, :, h, u, :], src[:, :, h, :]))
            return prs

        def mlp_groups(l):
            return [g_cols(w1_d[l], g * 512) for g in range(8)] + [g_rows(w2_d[l], g * 512) for g in range(8)]

        chunk_groups = ([g_cols(w_in_d, 2048), g_cols(w_in_d, 2560)] + [g_A(cb) for cb in range(4)]
                        + [g_cols(w_out_d, 0), g_cols(w_out_d, 512)] + mlp_groups(0))
        if nlayers > 1:
            chunk_groups += ([g_kv1, g_cols(w_qkv_d, 0), g_cols(w_qkv_d, 512), g_cols(w_o_d, 0), g_cols(w_o_d, 512)]
                             + mlp_groups(1))
        all_groups = chunk_groups * nchunks
        wstate = {"issued": 0, "next": 0}

        NG = len(chunk_groups)
        scr_d = nc.dram_tensor("wscr", [NG, 128, 4096], BF16, kind="Internal").ap()
        R_scr = [Reg() for _ in range(NG)]

        def w_issue_upto(n):
            while wstate["issued"] < min(n, len(all_groups)):
                i = wstate["issued"]
                s = i % NSLOT
                if i < NG:
                    S.dma("pool", all_groups[i](s), [], [R_W[s]], ("wc", s))
                else:
                    S.dma("sp", [(W_t[:, s, :], scr_d[i % NG])], [R_scr[i % NG]], [R_W[s]], ("w", s))
                wstate["issued"] += 1

        def w_writeback(i):
            if i < NG and nchunks > 1:
                s = i % NSLOT
                S.dma("sp", [(scr_d[i], W_t[:, s, :])], [R_W[s]], [R_scr[i]], ("scr", s))

        def w_next():
            i = wstate["next"]
            w_issue_upto(i + NSLOT)
            wstate["next"] += 1
            w_writeback(i)
            return i % NSLOT

        def w_next2():
            i = wstate["next"]
            w_issue_upto(i + NSLOT)
            wstate["next"] += 2
            w_writeback(i)
            w_writeback(i + 1)
            return [i % NSLOT, (i + 1) % NSLOT]

        def Wc(s, dc, a, b):
            return TV(W_t[:, s, dc * 512 + a: dc * 512 + b], [R_W[s]])

        def Wr(s, kb, a, b):
            return TV(W_t[:, s, kb * 1024 + a: kb * 1024 + b], [R_W[s]])

        def hT(dc, a=0, b=T):
            regs = [R_hT[t] for t in range(a // 128, (b + 127) // 128)]
            return TV(hT_t[:, dc, a:b], regs)

        def small():
            i = sm_ring.next()
            f = lambda c: TV(sm_t[:, i, c:c + 1], [R_sm[i]])
            f.idx = i
            return f

        def sm_ap(sm, a, b):
            return sm_t[:, sm.idx, a:b]

        def rstd_from(sm, src_col, dst_col, scale, eps_col):
            S.act(sm(dst_col), sm(src_col), AF.Ln, bias=cfc(eps_col), scale=scale)
            S.act(sm(dst_col), sm(dst_col), AF.Exp, scale=-0.5)

        def norm_to_hT(gbase):
            sm = small()

            def stats(tt):
                S.act(TV(junk_t[:], []), xs(tt), AF.Square, accum=sm(tt))
                S.act(sm(4 + tt), sm(tt), AF.Ln, bias=cfc(1), scale=1.0 / D)
                S.act(sm(4 + tt), sm(4 + tt), AF.Exp, scale=-0.5)

            def rest(tt):
                xb = tt % 2
                xn = TV(xn_t[:, xb, :], [R_xn[xb]])
                S.act(xn, xs(tt), AF.Copy, scale=sm(4 + tt))
                pb = psAll.next()
                pbf = ps_all[:, pb, :].bitcast(BF16)
                for dc in range(DC):
                    S.tr(TV(pbf[:, dc * 128:(dc + 1) * 128], [R_ps[pb]]),
                         TV(xn_t[:, xb, dc * 128:(dc + 1) * 128], [R_xn[xb]]), ident)
                gain = TV(pp_t[:, gbase:gbase + 8].unsqueeze(2).broadcast_to([128, 8, 128]), [R_pp])
                S.tt("dve", TV(hT_t[:, :, tt * 128:(tt + 1) * 128], [R_hT[tt]]),
                     TV(pbf.rearrange("p (c t) -> p c t", c=8), [R_ps[pb]]), gain, ALU.mult)

            stats(0)
            for tt in range(1, TT):
                stats(tt)
                rest(tt - 1)
            rest(TT - 1)

        def bc_load(row):
            s = bcr_ring.next()
            S.dma("sp", [(bcr_t[:, s, :], bc2_d[row:row + 1, :].partition_broadcast(128))], [], [R_bcr[s]], ("bc", s))
            return lambda a, b: TV(bcr_t[:, s, a:b], [R_bcr[s]])

        def post_norm_all(gp, final=None):
            sm = small()
            smi = sm(0).regs
            for tt in range(TT):
                S.act(TV(junk_t[:, :].rearrange("p (a b) -> p a b", a=2), []), ps2(tt), AF.Square, accum=sm(tt))
                S.act(sm(4 + tt), sm(tt), AF.Ln, bias=cfc(1), scale=1.0 / D)
                S.act(sm(4 + tt), sm(4 + tt), AF.Exp, scale=-0.5)
            for tt in range(TT):
                ti = tb_ring.next()
                tb3 = TV(tb_t[:, ti, :].rearrange("p (a b) -> p a b", a=2), [R_tb[ti]])
                tbf = TV(tb_t[:, ti, :], [R_tb[ti]])
                S.stt("dve", tb3, ps2(tt), sm(4 + tt), TV(gp(0, 1024).ap.rearrange("p (a b) -> p a b", a=2), gp(0, 1024).regs),
                      ALU.mult, ALU.mult)
                eng = "pool" if tt % 2 == 0 else "dve"
                if final is None:
                    S.tt(eng, xs(tt), xs(tt), tbf, ALU.add)
                else:
                    S.tt(eng, tbf, xs(tt), tbf, ALU.add)
                    final(tt, tbf)

        rope_pending = []

        def rope_block(pb, dest, bias):
            qi = qraw_ring.next()
            qr = ar(qi)
            if bias is None:
                S.copy("act", qr, psb(pb))
            else:
                S.act(qr, psb(pb), AF.Identity, bias=bias)

            def part_b():
                S.mm(psb(pb), rmat, qr)
                f1 = f_ring.next()
                f2 = f_ring.next()
                S.tt("dve", Ft(f1), psb(pb), Stab, ALU.mult)
                S.tt("pool", Ft(f2), qr, Ctab, ALU.mult)
                S.tt("dve", dest, Ft(f2), Ft(f1), ALU.add)
            rope_pending.append(part_b)

        def rope_flush(keep=0):
            while len(rope_pending) > keep:
                rope_pending.pop(0)()

        def rope_tables(ci):
            S.dma("sp", [(posi_t[:], pos_d[0:1, ci * T:(ci + 1) * T].partition_broadcast(128))], [], [R_posi], "pos")
            posi = TV(posi_t[:], [R_posi])
            ni = TV(ni_t[:], [R_ni])
            fa, fb = f_ring.next(), f_ring.next()
            S.copy("dve", Ft(fa), posi)
            for which, tab in ((0, Ctab), (1, Stab)):
                S.ts("dve", Ft(fb), Ft(fa), ppc(55), (np.pi / 2 if which == 0 else 0.0), ALU.mult, ALU.add)
                S.ts("dve", ni, Ft(fb), float(1 / (2 * np.pi)), None, ALU.mult)
                fc = f_ring.next()
                S.copy("dve", Ft(fc), ni)
                S.stt("dve", Ft(fb), Ft(fc), -C1, Ft(fb), ALU.mult, ALU.add)
                S.stt("dve", Ft(fb), Ft(fc), -C2, Ft(fb), ALU.mult, ALU.add)
                S.ts("dve", Ft(fb), Ft(fb), -3.1415925, 3.1415925, ALU.max, ALU.min)
                S.act(tab, Ft(fb), AF.Sin)

        def mlp(l, gpre, gpost_row, after_tt=None, mid=None):
            norm_to_hT(gpre)
            for g in range(8):
                s = w_next()
                for j in range(4):
                    fb = g * 4 + j
                    pb = psAll.next()
                    for dc in range(DC):
                        S.mm(psb(pb), Wc(s, dc, j * 128, (j + 1) * 128), hT(dc), start=(dc == 0), stop=(dc == DC - 1))
                    ri = rtmp_ring.next()
                    rt = TV(rtmp_t[:, ri, :], [R_rtmp[ri]])
                    S.act(rt, psb(pb), AF.Relu)
                    S.tt("pool" if fb % 2 == 0 else "dve", ar(fb), rt, rt, ALU.mult)
                if g == 2 and mid is not None:
                    mid()
            gp = bc_load(gpost_row)
            for g in range(6):
                s = w_next()
                for tt in range(TT):
                    for fh in range(2):
                        for kb in range(4):
                            S.mm(psb(tt * 2 + fh), ar(g * 4 + kb, tt * 128, (tt + 1) * 128), Wr(s, kb, fh * 512, (fh + 1) * 512),
                                 start=(g == 0 and kb == 0), stop=False)
            s67 = dict(zip((6, 7), w_next2()))
            for tt in range(TT):
                for fh in range(2):
                    for g in (6, 7):
                        for kb in range(4):
                            S.mm(psb(tt * 2 + fh), ar(g * 4 + kb, tt * 128, (tt + 1) * 128), Wr(s67[g], kb, fh * 512, (fh + 1) * 512),
                                 start=False, stop=(g == 7 and kb == 3))
            post_norm_all(gp, final=after_tt)

        def out_proj(gpost_row, bias_rows=None):
            gp = bc_load(gpost_row)
            ss_ = w_next2()
            for tt in range(TT):
                for fh in range(2):
                    s = ss_[fh]
                    for kb in range(8):
                        S.mm(psb(tt * 2 + fh), ar(MIX0 + kb, tt * 128, (tt + 1) * 128), Wc(s, kb, 0, 512),
                             start=(kb == 0), stop=(kb == 7 and bias_rows is None))
                    if bias_rows is not None:
                        p_, brow = bias_rows[fh]
                        S.mm(psb(tt * 2 + fh), TV(cst_t[p_:p_ + 1, 384:512], [R_cst]), brow, start=False, stop=True)
            post_norm_all(gp)

        pending_fin = []

        def attn_l0(h, ci):
            nkt = 4 * (ci + 1)
            OT = [4, 5]
            RS = [6, 7]
            pts = {}

            def q0_of(kt):
                j = kt - 4 * ci
                return 128 * j if j > 0 else 0

            def qk(kt):
                q0 = q0_of(kt)
                for c in range(2):
                    sb_ = 2 * (kt % 2) + c
                    kt_tv = TV(KT0_t[c * 64:(c + 1) * 64, h, kt * 128:(kt + 1) * 128], [R_KT0[h][kt // 4]])
                    S.mm(psb(sb_, q0, 512), kt_tv, ar(QT0 + h, q0, 512, c * 64, (c + 1) * 64))

            def ex(kt):
                q0 = q0_of(kt)
                for c in range(2):
                    sb_ = 2 * (kt % 2) + c
                    pi = pt_ring.next()
                    S.act(ar(pi, q0, 512), psb(sb_, q0, 512), AF.Exp, scale=0.125)
                    if kt - 4 * ci >= 0:
                        S.tt("pool", ar(pi, q0, q0 + 128), ar(pi, q0, q0 + 128), TV(tri_ge, [R_cst]), ALU.mult)
                    pts[(kt, c)] = pi

            def av(kt):
                q0 = q0_of(kt)
                for c in range(2):
                    pi = pts.pop((kt, c))
                    S.mm(psb(OT[c], q0, 512), TV(V0_t[:, kt, h * 128:(h + 1) * 128], [R_V0[kt]]), ar(pi, q0, 512),
                         start=(kt == 0), stop=(kt == nkt - 1))
                    S.mm(psb(RS[c], q0, 512), ones_b, ar(pi, q0, 512), start=(kt == 0), stop=(kt == nkt - 1))

            qk(0)
            ex(0)
            qk(1)
            ex(1)
            for n in range(nkt):
                S.batch_begin("pe")
                if n + 2 < nkt:
                    qk(n + 2)
                av(n)
                S.batch_end("pe")
                if n + 2 < nkt:
                    ex(n + 2)
                if n == 1 and pending_fin:
                    pending_fin.pop(0)()
            f0, f1, f2, f3 = f_ring.next(), f_ring.next(), f_ring.next(), f_ring.next()
            S.act(Ft(f0), psb(RS[0]), AF.Ln)
            S.act(Ft(f0), Ft(f0), AF.Exp, scale=-1.0)
            S.tt("dve", Ft(f1), psb(OT[0]), Ft(f0), ALU.mult)
            S.act(Ft(f2), psb(RS[1]), AF.Ln)
            S.act(Ft(f2), Ft(f2), AF.Exp, scale=-1.0)
            S.tt("dve", Ft(f3), psb(OT[1]), Ft(f2), ALU.mult)
            S.stt("dve", Ft(f1), Ft(f3), neglam, Ft(f1), ALU.mult, ALU.add)
            S.tt("pool", ar(OSQ), Ft(f1), Ft(f1), ALU.mult)

            def fin2():
                sb_ = psA.next()
                S.mm(psb(sb_), ones_b, ar(OSQ))
                S.act(Ft(f0), psb(sb_), AF.Ln, bias=cfc(2), scale=1.0 / 128)
                S.act(Ft(f0), Ft(f0), AF.Exp, scale=-0.5)
                S.stt("dve", ar(MIX0 + 4 + h), Ft(f1), wsub, Ft(f0), ALU.mult, ALU.mult)
            pending_fin.append(fin2)

        def mixer0(ci):
            norm_to_hT(0)
            s = w_next()
            for h in range(4):
                pb = psAll.next()
                for dc in range(DC):
                    S.mm(psb(pb), Wc(s, dc, h * 128, (h + 1) * 128), hT(dc), start=(dc == 0), stop=(dc == DC - 1))
                rope_block(pb, TV(KT0_t[:, h, ci * T:(ci + 1) * T], [R_KT0[h][ci]]), None)
                rope_flush(1)
            s = w_next()
            for tt in range(TT):
                pb = psAll.next()
                for dc in range(DC):
                    S.mm(psb(pb), hT(dc, tt * 128, (tt + 1) * 128), Wc(s, dc, 0, 512), start=(dc == 0), stop=(dc == DC - 1))
                S.copy("act" if tt % 2 else "dve", TV(V0_t[:, ci * 4 + tt, :], [R_V0[ci * 4 + tt]]), psb(pb))
                rope_flush(0)
            for cb in range(4):
                s = w_next()
                pbs = [psAll.next() for _ in range(4)]
                p_gb, p_gc, p_xc, p_q = pbs
                for j in (3, 0, 1, 2):
                    for dc in range(DC):
                        S.mm(psb(pbs[j]), Wc(s, dc, j * 128, (j + 1) * 128), hT(dc), start=(dc == 0), stop=(dc == DC - 1))
                    if j == 3:
                        rope_block(p_q, ar(QT0 + cb), None)
                rope_flush(0)
                fx, fg, fu, fa = f_ring.next(), f_ring.next(), f_ring.next(), f_ring.next()
                S.copy("act", Ft(fx), psb(p_xc))
                S.copy("act", Ft(fg), psb(p_gc))
                cr = TV(carry_t[:, cb, :], [R_carry[cb]])
                S.copy("pool", Ft(fu, 0, 2), cr)
                S.tt("pool", Ft(fu, 2, 514), Ft(fg), Ft(fx), ALU.mult)
                S.copy("pool", cr, Ft(fu, 512, 514))
                S.act(Ft(fa), Ft(fu, 2, 514), AF.Copy, scale=ppc(32 + 8 + cb))
                S.stt("dve", Ft(fa), Ft(fu, 1, 513), ppc(32 + 4 + cb), Ft(fa), ALU.mult, ALU.add)
                S.stt("dve", Ft(fa), Ft(fu, 0, 512), ppc(32 + cb), Ft(fa), ALU.mult, ALU.add)
                S.tt("dve", ar(MIX0 + cb), psb(p_gb), Ft(fa), ALU.mult)
            for h in range(4):
                attn_l0(h, ci)
            while pending_fin:
                pending_fin.pop(0)()
            out_proj(0)

        es3 = es_t[:, :].rearrange("p (i two) -> p i two", two=2)

        def attn_l1(ci):
            st_ring = Ring([0, 1])
            pair_of = {}
            def front(qb, kvh, par, banks=None):
                gblk = ci * TT + qb
                p0, p1 = par * 64, (par + 1) * 64
                rq = TV(ar_t[p0:p1, QT0 + kvh * 4:QT0 + kvh * 4 + 4, qb * 128:(qb + 1) * 128],
                        [R_ar[QT0 + kvh * 4 + i] for i in range(4)])
                kblocks = ([(qb, 768)] if gblk > 0 else []) + [(qb + 1, 640)]
                pts = []
                for n_k, (kb_, mcol) in enumerate(kblocks):
                    sb_ = st_ring.next() if banks is None else banks[n_k]
                    st3 = TV(ps_all[:, sb_, :].rearrange("p (i q) -> p i q", i=4), [R_ps[sb_]])
                    S.mm(st3, TV(KT1_t[p0:p1, kvh, kb_ * 128:(kb_ + 1) * 128], [R_KT1[kvh][kb_]]), rq, start=True, stop=False)
                    S.mm(st3, ident, TV(cst_t[:, mcol:mcol + 128].unsqueeze(1).broadcast_to([128, 4, 128]), [R_cst]),
                         start=False, stop=True)
                    pi = pt_ring.next()
                    S.act(ar(pi), psb(sb_), AF.Exp, scale=0.125)
                    pts.append((kb_, pi))
                return pts

            def back(qb, kvh, par, pts):
                p0, p1 = par * 64, (par + 1) * 64
                ob, rb = pair_of[(qb, kvh, par)]
                for n_, (kb_, pi) in enumerate(pts):
                    S.mm(psb(ob), TV(V1_t[:, kb_, kvh * 128:(kvh + 1) * 128], [R_V1[kb_]]), ar(pi),
                         start=(n_ == 0), stop=(n_ == len(pts) - 1))
                for n_, (kb_, pi) in enumerate(pts):
                    S.mm(psb(rb), ones_b, ar(pi), start=(n_ == 0), stop=False)
                ep_, eblk = ESROW[(kvh, par)]
                S.mm(psb(rb), TV(cst_t[ep_:ep_ + 1, 384:512], [R_cst]), TV(ar_t[ep_:ep_ + 1, eblk, :], [R_ar[eblk]]),
                     start=False, stop=True)
                fd = f_ring.next()
                den = TV(F_t[p0:p1, fd, 0:512].rearrange("p (i q) -> p i q", i=4), [R_F[fd]])
                S.act(Ft(fd, 0, 512, p0, p1), psb(rb, 0, 512, p0, p1), AF.Ln)

                def tail():
                    S.act(Ft(fd, 0, 512, p0, p1), Ft(fd, 0, 512, p0, p1), AF.Exp, scale=-1.0)
                    S.tt("dve", TV(ar_t[p0:p1, MIX0 + kvh * 4:MIX0 + kvh * 4 + 4, qb * 128:(qb + 1) * 128],
                                   [R_ar[MIX0 + kvh * 4 + i] for i in range(4)]),
                         TV(ps_all[p0:p1, ob, :].rearrange("p (i q) -> p i q", i=4), [R_ps[ob]]), den, ALU.mult)
                return tail

            its = [(qb, kvh, par) for kvh in range(2) for qb in range(TT) for par in range(2)]
            LAG = 3
            fr = {}
            for i in range(min(LAG, len(its))):
                fr[i] = front(*its[i], banks=(2 * i, 2 * i + 1))
            for i, it_ in enumerate(its):
                pair_of[it_] = (2 + 2 * (i % 3), 3 + 2 * (i % 3))
            def main():
                prev_tail = None
                for i in range(len(its)):
                    S.batch_begin("pe")
                    if i + LAG < len(its):
                        fr[i + LAG] = front(*its[i + LAG])
                    tl = back(*its[i], fr.pop(i))
                    S.batch_end("pe")
                    if prev_tail is not None:
                        prev_tail()
                    prev_tail = tl
                prev_tail()
            return main

        ESROW = {(0, 0): (0, 28), (0, 1): (32, 28), (1, 0): (64, 28), (1, 1): (0, 29)}
        BOROW = [(32, 29), (64, 29)]

        def mixer1(ci):
            bo = bc_load(4)
            for (kvh, par), (p_, blk) in ESROW.items():
                S.copy("dve", TV(ar_t[p_:p_ + 1, blk, :].rearrange("p (i q) -> p i q", i=4), [R_ar[blk]]),
                       TV(es3[p_:p_ + 1, kvh * 4:(kvh + 1) * 4, par].unsqueeze(2).broadcast_to([1, 4, 128]), [R_es]))
            bo_rows = []
            for fh, (p_, blk) in enumerate(BOROW):
                src = bo(fh * 512, (fh + 1) * 512)
                S.copy("dve", TV(ar_t[p_:p_ + 1, blk, :], [R_ar[blk]]), TV(src.ap[p_:p_ + 1, :], src.regs))
                bo_rows.append((p_, TV(ar_t[p_:p_ + 1, blk, :], [R_ar[blk]])))
            norm_to_hT(16)
            s = w_next()
            skv = s
            for kvh in range(2):
                pb = psAll.next()
                for dc in range(DC):
                    S.mm(psb(pb), Wc(s, dc, kvh * 128, (kvh + 1) * 128), hT(dc), start=(dc == 0), stop=(dc == DC - 1))
                rope_block(pb, TV(KT1_t[:, kvh, 128:640], [R_KT1[kvh][i] for i in range(1, 5)]), ppc(53 + kvh))
                rope_flush(1)
            for tt in range(TT):
                pb = psAll.next()
                for dc in range(DC):
                    S.mm(psb(pb, 0, 256), hT(dc, tt * 128, (tt + 1) * 128), Wc(skv, dc, 256, 512), start=(dc == 0), stop=(dc == DC - 1))
                S.tt("dve", TV(V1_t[:, 1 + tt, :], [R_V1[1 + tt]]), psb(pb, 0, 256), bvb, ALU.add)
                rope_flush(0)
            for qh in range(2):
                s = w_next()
                for j in range(4):
                    blk = qh * 4 + j
                    pb = psAll.next()
                    for dc in range(DC):
                        S.mm(psb(pb), Wc(s, dc, j * 128, (j + 1) * 128), hT(dc), start=(dc == 0), stop=(dc == DC - 1))
                    rope_block(pb, ar(QT0 + blk), ppc(45 + blk))
                    rope_flush(1)
                    if blk == 5:
                        attn_main = attn_l1(ci)
            rope_flush(0)
            attn_main()
            for kvh in range(2):
                S.copy("pool", TV(KT1_t[:, kvh, 0:128], [R_KT1[kvh][0]]), TV(KT1_t[:, kvh, 512:640], [R_KT1[kvh][4]]))
            S.copy("pool", TV(V1_t[:, 0, :], [R_V1[0]]), TV(V1_t[:, 4, :], [R_V1[4]]))
            out_proj(1, bias_rows=bo_rows)

        stores = []

        def load_x(ci, tt):
            r0 = ci * T + tt * 128
            S.dma("sp", [(xs_t[:, tt, :], x_d[r0:r0 + 128, :])], [], [R_xs[tt]], ("x", tt))

        for tt in range(TT):
            load_x(0, tt)
        for ci in range(nchunks):
            def after_tt(tt, src, ci=ci):
                r0 = ci * T + tt * 128
                stores.append(S.dma("sp", [(out_d[r0:r0 + 128, :], src.ap)], list(src.regs), [], ("y", tt)))
                if ci + 1 < nchunks:
                    load_x(ci + 1, tt)
            if ci == 0:
                rope_tables(0)
            mixer0(ci)
            if nlayers > 1:
                mlp(0, 8, 2)
                mixer1(ci)
                mlp(1, 24, 3, after_tt, mid=((lambda ci=ci: rope_tables(ci + 1)) if ci + 1 < nchunks else None))
            else:
                mlp(0, 8, 2, after_tt)
        S.emit("sp", stores[-TT:])
    return nc


def _host_consts():
    cst = np.zeros((128, 896), np.float32)
    k = np.arange(128)[:, None]
    q = np.arange(128)[None, :]
    cst[:, 0:128] = (k == q)
    cst[:, 128:256] = (q >= k)
    cst[:, 256:384] = (k > q)
    cst[:, 384:512] = 1.0
    rm = np.zeros((128, 128), np.float32)
    for m in range(128):
        mm_ = m % 64
        if mm_ < 8:
            rm[m + 8, m] = -1.0
        elif mm_ < 16:
            rm[m - 8, m] = 1.0
    cst[:, 512:640] = rm
    cst[:, 640:768] = np.where(q >= k, 0.0, -30000.0)
    cst[:, 768:896] = np.where(k > q, 0.0, -30000.0)
    return cst


def _prep(inputs, nlayers=2):
    f = lambda a: np.ascontiguousarray(np.asarray(a, dtype=np.float32))
    pm = lambda v: np.asarray(v, np.float32).reshape(8, 128).T
    pp = np.zeros((128, 56), np.float32)
    pp[:, 0:8] = pm(inputs["norm_pre_mix"][0])
    pp[:, 8:16] = pm(inputs["norm_pre_mlp"][0])
    pp[:, 16:24] = pm(inputs["norm_pre_mix"][1])
    pp[:, 24:32] = pm(inputs["norm_pre_mlp"][1])
    cw = np.asarray(inputs["even_conv_w"][0], np.float32)
    for i in range(3):
        pp[:, 32 + i * 4:32 + i * 4 + 4] = cw[i].reshape(4, 128).T
    pp[:, 44] = np.asarray(inputs["even_subln_w"][0], np.float32)
    bq = np.asarray(inputs["odd_b_qkv"][0], np.float32)
    pp[:, 45:53] = pm(bq[0:1024])
    pp[:, 53] = np.tile(bq[1024:1088], 2)
    pp[:, 54] = np.tile(bq[1088:1152], 2)
    invf = (np.float32(500000.0) ** (-(np.arange(0, 16, 2, dtype=np.float32)) / np.float32(16))).astype(np.float32)
    col = np.zeros(128, np.float32)
    for p in range(128):
        if p % 64 < 16:
            col[p] = invf[p % 8]
    pp[:, 55] = col
    bc = np.zeros((1, 528), np.float32)
    bc[0, 0:64] = inputs["even_lambda_q1"][0]
    bc[0, 64:128] = inputs["even_lambda_k1"][0]
    bc[0, 128:192] = inputs["even_lambda_q2"][0]
    bc[0, 192:256] = inputs["even_lambda_k2"][0]
    bc[0, 256:272] = inputs["odd_sinks"][0]
    bv = bq[1152:1280]
    bc[0, 272:528] = np.concatenate([bv[0:64], bv[0:64], bv[64:128], bv[64:128]])
    bc2 = np.stack([np.asarray(inputs["norm_post_mix"][0], np.float32), np.asarray(inputs["norm_post_mix"][1], np.float32),
                    np.asarray(inputs["norm_post_mlp"][0], np.float32), np.asarray(inputs["norm_post_mlp"][1], np.float32),
                    np.asarray(inputs["odd_b_o"][0], np.float32)])
    shared = {
        "w_in": f(inputs["even_w_in"][0]), "w_out": f(inputs["even_w_out"][0]),
        "w_qkv": f(inputs["odd_w_qkv"][0]), "w_o": f(inputs["odd_w_o"][0]),
        "w1": f(inputs["mlp_w1"]), "w2": f(inputs["mlp_w2"]),
        "pp": pp, "bc": bc, "bc2": f(bc2), "cst": _host_consts(),
    }
    x = np.asarray(inputs["x"], np.float32)
    pos = np.asarray(inputs["positions"], np.int32)
    in_maps = []
    for b in range(8):
        m = dict(shared)
        m["x"] = np.ascontiguousarray(x[b])
        m["pos"] = np.ascontiguousarray(pos[b][None, :])
        in_maps.append(m)
    return in_maps


def kernel(**inputs):
    in_maps = _prep(inputs)
    nc = build()
    res = run_bass_kernel_spmd(nc, in_maps, core_ids=list(range(8)))
    return np.stack([np.asarray(r["out"], np.float32) for r in res.results], axis=0)
```
